# Optimizing a Trainium2 kernel written in Bass

```python
import math
import jax, jax.numpy as jnp
from jax import lax
import numpy as np

D_MODEL = 1024
BATCH = 4
SEQ = 4096
DEPTH = 2

N_MEM = 256
RMS_EPS = 1e-6
MAX_POS_OFFSET = 1024
MASK_VALUE = -1e30

SSM_GROUPS = 32
SSM_GROUP_CH = 16
SSM_WIDTH = SSM_GROUPS * SSM_GROUP_CH
SSM_STATE = 64

MLA_HEADS = 8
MLA_Q_RANK = 512
MLA_KV_RANK = 256
MLA_NOPE = 64
MLA_ROPE = 32
MLA_V = 64
MLA_WIDTH = MLA_HEADS * MLA_V
ROPE_THETA = 10000.0
Q_BLOCK = 128

HG_HEADS = 4
HG_DK = 128
HG_DV = 128
HG_WIDTH = HG_HEADS * HG_DK
HG_CHUNK = 64

X_HEADS = 4
X_HEAD_DIM = 128
X_WIDTH = X_HEADS * X_HEAD_DIM

D_FF = -(-(8 * D_MODEL) // (3 * 256)) * 256

N_BRANCH = 3
IN_SPLITS = [SSM_WIDTH, MLA_Q_RANK, MLA_KV_RANK, MLA_ROPE,
             HG_HEADS * HG_DK, HG_HEADS * HG_DK, HG_HEADS * HG_DV, HG_HEADS * HG_DV,
             N_BRANCH * D_MODEL]
D_IN = sum(IN_SPLITS)

kernel_name = "hybrid_s5_mla_hgrn2_gated_block"


def rmsnorm(x, g):
    xf = x.astype(jnp.float32)
    y = xf * lax.rsqrt(jnp.mean(xf * xf, axis=-1, keepdims=True) + RMS_EPS)
    return (y * g.astype(jnp.float32)).astype(x.dtype)


def rope_tables(positions):
    half = MLA_ROPE // 2
    inv_freq = ROPE_THETA ** (-jnp.arange(half, dtype=jnp.float32) / half)
    ang = positions.astype(jnp.float32)[..., None] * inv_freq
    return jnp.cos(ang), jnp.sin(ang)


def apply_rope(x, cos, sin):
    xf = x.astype(jnp.float32)
    x1, x2 = jnp.split(xf, 2, axis=-1)
    return jnp.concatenate([x1 * cos - x2 * sin, x2 * cos + x1 * sin], axis=-1).astype(x.dtype)


def s5_mixer(u, lam_re, lam_im, b_re, b_im, c_re, c_im, d_skip, log_step, w_glu):
    bsz, s, _ = u.shape
    uf = u.astype(jnp.float32).reshape(bsz, s, SSM_GROUPS, SSM_GROUP_CH)
    lam = lax.complex(lam_re.astype(jnp.float32), lam_im.astype(jnp.float32))
    step = jnp.exp(log_step.astype(jnp.float32))[:, None]
    lam_bar = jnp.exp(lam * step)
    b_mat = lax.complex(b_re.astype(jnp.float32), b_im.astype(jnp.float32))
    b_bar = ((lam_bar - 1.0) / lam)[..., None] * b_mat
    bu = jnp.einsum('bsgh,gph->bsgp', uf.astype(jnp.complex64), b_bar)
    a = jnp.broadcast_to(lam_bar, bu.shape)

    def combine(left, right):
        a_l, b_l = left
        a_r, b_r = right
        return a_r * a_l, a_r * b_l + b_r

    _, states = lax.associative_scan(combine, (a, bu), axis=1)
    c_mat = lax.complex(c_re.astype(jnp.float32), c_im.astype(jnp.float32))
    y = jnp.real(jnp.einsum('bsgp,ghp->bsgh', states, c_mat)) + d_skip.astype(jnp.float32) * uf
    y = jax.nn.gelu(y.reshape(bsz, s, SSM_WIDTH)).astype(u.dtype)
    z_out, z_gate = jnp.split(y @ w_glu, 2, axis=-1)
    return z_out * jax.nn.sigmoid(z_gate)


def blocked_causal_attention(q, k, v, scale):
    bsz, s, h, dqk = q.shape
    nb = s // Q_BLOCK
    qb = q.reshape(bsz, nb, Q_BLOCK, h, dqk).transpose(1, 0, 2, 3, 4)
    kpos = jnp.arange(s)

    def one_block(args):
        q_blk, start = args
        sc = jnp.einsum('bqhd,bkhd->bhqk', q_blk, k, preferred_element_type=jnp.float32) * scale
        qpos = start + jnp.arange(Q_BLOCK)
        sc = jnp.where(kpos[None, :] <= qpos[:, None], sc, MASK_VALUE)
        p = jax.nn.softmax(sc, axis=-1).astype(v.dtype)
        return jnp.einsum('bhqk,bkhd->bqhd', p, v)

    ob = lax.map(one_block, (qb, jnp.arange(nb) * Q_BLOCK))
    return ob.transpose(1, 0, 2, 3, 4).reshape(bsz, s, h, v.shape[-1])


def mla_mixer(q_lat, kv_lat, k_rope, cos, sin, q_norm, kv_norm, w_uq, w_ukv, w_o):
    bsz, s, _ = q_lat.shape
    q = (rmsnorm(q_lat, q_norm) @ w_uq).reshape(bsz, s, MLA_HEADS, MLA_NOPE + MLA_ROPE)
    q_nope, q_pe = q[..., :MLA_NOPE], q[..., MLA_NOPE:]
    q_pe = apply_rope(q_pe, cos[:, :, None, :], sin[:, :, None, :])
    kv = (rmsnorm(kv_lat, kv_norm) @ w_ukv).reshape(bsz, s, MLA_HEADS, MLA_NOPE + MLA_V)
    k_nope, v = kv[..., :MLA_NOPE], kv[..., MLA_NOPE:]
    k_pe = apply_rope(k_rope, cos, sin)
    k = jnp.concatenate([k_nope, jnp.broadcast_to(k_pe[:, :, None, :], (bsz, s, MLA_HEADS, MLA_ROPE))], axis=-1)
    q = jnp.concatenate([q_nope, q_pe], axis=-1)
    o = blocked_causal_attention(q, k, v, 1.0 / math.sqrt(MLA_NOPE + MLA_ROPE))
    return o.reshape(bsz, s, MLA_WIDTH) @ w_o


def hgrn2_mixer(q, f_logit, i_in, g, lb, g_norm, w_o):
    bsz, s, _ = q.shape
    n_chunks = s // HG_CHUNK

    def to_chunks(t):
        return t.reshape(bsz, n_chunks, HG_CHUNK, HG_HEADS, -1).transpose(1, 0, 3, 2, 4)

    lbf = lb.astype(jnp.float32)
    sig = jax.nn.sigmoid(f_logit.astype(jnp.float32))
    f = lbf + (1.0 - lbf) * sig
    log_f = jnp.log(f)
    k = 1.0 - f
    xs = (to_chunks(jax.nn.silu(q.astype(jnp.float32))), to_chunks(k),
          to_chunks(i_in.astype(jnp.float32)), to_chunks(log_f))
    causal = jnp.tril(jnp.ones((HG_CHUNK, HG_CHUNK), dtype=bool))[:, :, None]

    def chunk_step(state, chunk):
        q_n, k_n, v_n, lf_n = chunk
        b = jnp.cumsum(lf_n, axis=2)
        diff = b[:, :, :, None, :] - b[:, :, None, :, :]
        decay = jnp.where(causal, jnp.exp(jnp.where(causal, diff, 0.0)), 0.0)
        attn = jnp.einsum('bhtd,bhsd,bhtsd->bhts', q_n, k_n, decay)
        o_n = jnp.einsum('bhts,bhse->bhte', attn, v_n) + jnp.einsum('bhtd,bhde->bhte', q_n * jnp.exp(b), state)
        b_last = b[:, :, -1, :]
        k_dec = k_n * jnp.exp(b_last[:, :, None, :] - b)
        state = jnp.exp(b_last)[..., None] * state + jnp.einsum('bhsd,bhse->bhde', k_dec, v_n)
        return state, o_n

    s0 = jnp.zeros((bsz, HG_HEADS, HG_DK, HG_DV), jnp.float32)
    _, o = lax.scan(chunk_step, s0, xs)
    o = o.transpose(1, 0, 3, 2, 4).reshape(bsz, s, HG_HEADS, HG_DV)
    gate = g.astype(jnp.float32).reshape(bsz, s, HG_HEADS, HG_DV)
    o = rmsnorm(o, g_norm) * jax.nn.silu(gate)
    return o.reshape(bsz, s, HG_HEADS * HG_DV).astype(g.dtype) @ w_o


def memory_cross_attention(h, mem_n, w_q, w_kv, w_o):
    bsz, s, _ = h.shape
    m = mem_n.shape[1]
    q = (h @ w_q).reshape(bsz, s, X_HEADS, X_HEAD_DIM)
    k, v = jnp.split(mem_n @ w_kv, 2, axis=-1)
    k = k.reshape(bsz, m, X_HEADS, X_HEAD_DIM)
    v = v.reshape(bsz, m, X_HEADS, X_HEAD_DIM)
    sc = jnp.einsum('bqhd,bkhd->bhqk', q, k, preferred_element_type=jnp.float32) / math.sqrt(X_HEAD_DIM)
    p = jax.nn.softmax(sc, axis=-1).astype(v.dtype)
    o = jnp.einsum('bhqk,bkhd->bqhd', p, v)
    return o.reshape(bsz, s, X_WIDTH) @ w_o


def swiglu_ffn(h, w_gate_up, w_down):
    gt, up = jnp.split(h @ w_gate_up, 2, axis=-1)
    return (jax.nn.silu(gt) * up) @ w_down


def setup_inputs(seed: int = 0) -> dict:
    key = jax.random.key(seed)
    keys = list(jax.random.split(key, 48))
    L = DEPTH

    def nrm(shape, scale):
        return scale * jax.random.normal(keys.pop(), shape, jnp.float32)

    def gain(shape):
        return 1.0 + 0.05 * jax.random.normal(keys.pop(), shape, jnp.float32)

    x = nrm((BATCH, SEQ, D_MODEL), 1.0)
    mem = nrm((BATCH, N_MEM, D_MODEL), 1.0)
    offsets = jax.random.randint(keys.pop(), (BATCH, 1), 0, MAX_POS_OFFSET, dtype=jnp.int32)
    positions = (offsets + jnp.arange(SEQ, dtype=jnp.int32)[None, :]).astype(jnp.int32)
    state_idx = jnp.arange(SSM_STATE, dtype=jnp.float32)
    return {
        "x": x,
        "mem": mem,
        "positions": positions,
        "norm_mix": gain((L, D_MODEL)),
        "w_in": nrm((L, D_MODEL, D_IN), D_MODEL ** -0.5),
        "ssm_lam_re": -0.5 + nrm((L, SSM_GROUPS, SSM_STATE), 0.01),
        "ssm_lam_im": math.pi * state_idx + nrm((L, SSM_GROUPS, SSM_STATE), 0.01),
        "ssm_b_re": nrm((L, SSM_GROUPS, SSM_STATE, SSM_GROUP_CH), (2.0 * SSM_GROUP_CH) ** -0.5),
        "ssm_b_im": nrm((L, SSM_GROUPS, SSM_STATE, SSM_GROUP_CH), (2.0 * SSM_GROUP_CH) ** -0.5),
        "ssm_c_re": nrm((L, SSM_GROUPS, SSM_GROUP_CH, SSM_STATE), (2.0 * SSM_STATE) ** -0.5),
        "ssm_c_im": nrm((L, SSM_GROUPS, SSM_GROUP_CH, SSM_STATE), (2.0 * SSM_STATE) ** -0.5),
        "ssm_d": nrm((L, SSM_GROUPS, SSM_GROUP_CH), 1.0),
        "ssm_log_step": jax.random.uniform(keys.pop(), (L, SSM_GROUPS), jnp.float32, math.log(1e-3), math.log(1e-1)),
        "ssm_w_glu": nrm((L, SSM_WIDTH, 2 * D_MODEL), SSM_WIDTH ** -0.5),
        "mla_q_norm": gain((L, MLA_Q_RANK)),
        "mla_kv_norm": gain((L, MLA_KV_RANK)),
        "mla_w_uq": nrm((L, MLA_Q_RANK, MLA_HEADS * (MLA_NOPE + MLA_ROPE)), MLA_Q_RANK ** -0.5),
        "mla_w_ukv": nrm((L, MLA_KV_RANK, MLA_HEADS * (MLA_NOPE + MLA_V)), MLA_KV_RANK ** -0.5),
        "mla_w_o": nrm((L, MLA_WIDTH, D_MODEL), MLA_WIDTH ** -0.5),
        "hg_lb": nrm((L, HG_HEADS * HG_DK), 1.0),
        "hg_g_norm": gain((L, HG_DV)),
        "hg_w_o": nrm((L, HG_HEADS * HG_DV, D_MODEL), (HG_HEADS * HG_DV) ** -0.5),
        "w_out": nrm((L, D_MODEL, D_MODEL), D_MODEL ** -0.5),
        "norm_cross": gain((L, D_MODEL)),
        "norm_mem": gain((L, D_MODEL)),
        "x_w_q": nrm((L, D_MODEL, X_WIDTH), D_MODEL ** -0.5),
        "x_w_kv": nrm((L, D_MODEL, 2 * X_WIDTH), D_MODEL ** -0.5),
        "x_w_o": nrm((L, X_WIDTH, D_MODEL), X_WIDTH ** -0.5),
        "norm_ffn": gain((L, D_MODEL)),
        "ffn_w_gate_up": nrm((L, D_MODEL, 2 * D_FF), D_MODEL ** -0.5),
        "ffn_w_down": nrm((L, D_FF, D_MODEL), D_FF ** -0.5),
        "norm_final": gain((D_MODEL,)),
    }


def reference(x, mem, positions, norm_mix, w_in, ssm_lam_re, ssm_lam_im, ssm_b_re, ssm_b_im,
              ssm_c_re, ssm_c_im, ssm_d, ssm_log_step, ssm_w_glu, mla_q_norm, mla_kv_norm,
              mla_w_uq, mla_w_ukv, mla_w_o, hg_lb, hg_g_norm, hg_w_o, w_out, norm_cross, norm_mem,
              x_w_q, x_w_kv, x_w_o, norm_ffn, ffn_w_gate_up, ffn_w_down, norm_final):
    bsz, s, _ = x.shape
    cos, sin = rope_tables(positions)
    lb_p = jax.nn.softmax(hg_lb.astype(jnp.float32), axis=0)
    lower_bounds = jnp.cumsum(lb_p, axis=0) - lb_p[0:1]
    split_at = np.cumsum(IN_SPLITS)[:-1].tolist()

    for l in range(DEPTH):
        h = rmsnorm(x, norm_mix[l])
        (u_ssm, q_lat, kv_lat, k_rope, hg_q, hg_f, hg_i, hg_g, gate_logits) = jnp.split(h @ w_in[l], split_at, axis=-1)
        y_ssm = s5_mixer(u_ssm, ssm_lam_re[l], ssm_lam_im[l], ssm_b_re[l], ssm_b_im[l],
                         ssm_c_re[l], ssm_c_im[l], ssm_d[l], ssm_log_step[l], ssm_w_glu[l])
        y_mla = mla_mixer(q_lat, kv_lat, k_rope, cos, sin, mla_q_norm[l], mla_kv_norm[l],
                          mla_w_uq[l], mla_w_ukv[l], mla_w_o[l])
        y_hg = hgrn2_mixer(hg_q, hg_f, hg_i, hg_g, lower_bounds[l], hg_g_norm[l], hg_w_o[l])
        gates = jax.nn.sigmoid(gate_logits.astype(jnp.float32)).reshape(bsz, s, N_BRANCH, D_MODEL)
        merged = (gates[:, :, 0] * y_ssm.astype(jnp.float32)
                  + gates[:, :, 1] * y_mla.astype(jnp.float32)
                  + gates[:, :, 2] * y_hg.astype(jnp.float32)).astype(x.dtype)
        x = x + merged @ w_out[l]
        x = x + memory_cross_attention(rmsnorm(x, norm_cross[l]), rmsnorm(mem, norm_mem[l]),
                                       x_w_q[l], x_w_kv[l], x_w_o[l])
        x = x + swiglu_ffn(rmsnorm(x, norm_ffn[l]), ffn_w_gate_up[l], ffn_w_down[l])
    return rmsnorm(x, norm_final)
```

```python
import contextlib
import math
import numpy as np
import concourse.bass as bass
import concourse.mybir as mybir
from concourse.bass_utils import run_bass_kernel_spmd

F32 = mybir.dt.float32
BF16 = mybir.dt.bfloat16
I32 = mybir.dt.int32
AF = mybir.ActivationFunctionType
ALU = mybir.AluOpType
AX = mybir.AxisListType


class Prog:
    ENGS = ("pe", "act", "dve", "pool", "sp")

    def __init__(self, nc):
        self.nc = nc
        self.gstack = contextlib.ExitStack()
        self.esem = {e: self.gstack.enter_context(nc.semaphore(f"se_{e}")) for e in self.ENGS}
        self.ecnt = {e: 0 for e in self.ENGS}
        self.dsem = []
        self.prev_final = []
        self.n_t = 0
        self.ext_in = []
        self.begin()

    def begin(self):
        self.ops = []
        self.last_w = {}
        self.readers = {}
        self.stack = contextlib.ExitStack()

    def sb(self, shape, dt=F32, name=None):
        self.n_t += 1
        return self.stack.enter_context(self.nc.sbuf_tensor(f"{name or 't'}_{self.n_t}", list(shape), dt))

    def ps(self, shape, dt=F32, name=None):
        self.n_t += 1
        return self.stack.enter_context(self.nc.psum_tensor(f"{name or 'p'}_{self.n_t}", list(shape), dt))

    def dram(self, name, shape, dt, kind):
        return self.nc.dram_tensor(name, list(shape), dt, kind=kind).ap()

    def op(self, eng, fn, reads=(), writes=(), dma_key=None):
        idx = len(self.ops)
        deps = set()
        for k in reads:
            w = self.last_w.get(k)
            if w is not None:
                deps.add(w)
        for k in writes:
            w = self.last_w.get(k)
            if w is not None:
                deps.add(w)
            for r in self.readers.get(k, ()):
                deps.add(r)
        deps.discard(idx)
        o = dict(eng=eng, fn=fn, deps=deps, dma_key=dma_key, sig=False, idx=idx)
        self.ops.append(o)
        for k in reads:
            self.readers.setdefault(k, []).append(idx)
        for k in writes:
            self.last_w[k] = idx
            self.readers[k] = []
        return idx

    def dma(self, eng, out, in_, reads=(), writes=(), key=None, **kw):
        assert key is not None
        return self.op(eng, lambda e: e.dma_start(out=out, in_=in_, **kw), reads, writes, dma_key=key)

    def end(self):
        nc = self.nc
        ops = self.ops
        for o in ops:
            if o["eng"] == "pe" and o["dma_key"] is None:
                o["deps"] = {d for d in o["deps"] if not (ops[d]["eng"] == "pe" and ops[d]["dma_key"] is None)}
            for d in o["deps"]:
                ops[d]["sig"] = True
        last = {}
        for o in ops:
            if o["dma_key"] is None:
                last[o["eng"]] = o
        for o in last.values():
            o["sig"] = True
        kmap = {}
        for o in ops:
            if o["dma_key"] is not None:
                k = o["dma_key"]
                if k not in kmap:
                    kmap[k] = len(kmap)
                    if kmap[k] >= len(self.dsem):
                        self.dsem.append([self.gstack.enter_context(nc.semaphore(f"sd_{len(self.dsem)}")), 0])
                ent = self.dsem[kmap[k]]
                ent[1] += 16
                o["sem"] = ent[0]
                o["val"] = ent[1]
            elif o["sig"]:
                self.ecnt[o["eng"]] += 1
                o["sem"] = self.esem[o["eng"]]
                o["val"] = self.ecnt[o["eng"]]
        per_eng = {e: [o for o in ops if o["eng"] == e] for e in self.ENGS}
        prev_final = self.prev_final
        final = [(self.esem[e], self.ecnt[e]) for e in self.ENGS if self.ecnt[e] > 0] + [(h, c) for h, c in self.dsem if c > 0]

        def run(eng_name, eng):
            known = {}
            for sm, v in prev_final:
                eng.wait_ge(sm, v)
                known[sm] = v
            for o in per_eng[eng_name]:
                need = {}
                for d in o["deps"]:
                    od = ops[d]
                    sm, v = od["sem"], od["val"]
                    if known.get(sm, 0) < v and need.get(sm, 0) < v:
                        need[sm] = v
                for sm, v in need.items():
                    eng.wait_ge(sm, v)
                    known[sm] = v
                ins = o["fn"](eng)
                if o["dma_key"] is not None:
                    ins.then_inc(o["sem"], 16)
                elif o["sig"]:
                    ins.then_inc(o["sem"], 1)
            if eng_name == "sp":
                for sm, v in final:
                    if known.get(sm, 0) < v:
                        eng.wait_ge(sm, v)

        with nc.Block() as block:
            @block.tensor
            def _(e):
                run("pe", e)

            @block.scalar
            def _(e):
                run("act", e)

            @block.vector
            def _(e):
                run("dve", e)

            @block.gpsimd
            def _(e):
                run("pool", e)

            @block.sync
            def _(e):
                run("sp", e)
        self.prev_final = final
        self.stack.close()
        self.begin()

    def finish(self):
        self.gstack.close()


class IO:
    def __init__(self, P, prefix="", over=None):
        self.P = P
        self.prefix = prefix
        self.over = over or {}

    def __call__(self, name, shape, dt, kind):
        if name in self.over:
            return self.over[name]
        if kind == "ExternalInput":
            self.P.ext_in.append((self.prefix, name))
        return self.P.dram(self.prefix + name, shape, dt, kind)


def _own(P, io):
    if P is None:
        P = Prog(bass.Bass("TRN2", target_bir_lowering=False))
        return P, IO(P), True
    return P, io, False


EPS = 1e-6
D = 1024
NA = 3360
DFF = 2816


class Builder:
    def __init__(self, P, T=1024):
        self.nc = P.nc
        self.P = P
        self.T = T
        self.psb = [P.ps([128, 512], F32, name=f"bank{i}") for i in range(8)]
        self.pair_i = 0
        self.ones = P.sb([128, 128], BF16, name="ones")
        P.op("dve", lambda e: e.memset(self.ones[:], 1.0), [], ["ones"])
        self.sq = [P.sb([128, T], BF16, name=f"sq{i}") for i in range(2)]
        self.rstd = P.sb([128, T], F32, name="rstd")
        self.evi = 0

    def pair(self):
        i = self.pair_i
        self.pair_i = (i + 1) % 4
        return [2 * i, 2 * i + 1]

    def bank(self):
        i = self.pair_i
        self.pair_i = (i + 1) % 4
        return 2 * i

    def ev_eng(self):
        self.evi += 1
        return "act" if self.evi % 2 else "dve"

    def copy(self, eng, out, in_, reads, writes):
        if eng == "act":
            self.P.op("act", lambda e: e.activation(out=out, in_=in_, func=AF.Copy), reads, writes)
        else:
            self.P.op(eng, lambda e: e.tensor_copy(out=out, in_=in_), reads, writes)

    def rmsnorm(self, X, xk, KC, T, gcol, gkey, H, hk, nfeat, hout=None, after=None):
        P = self.P
        nb = (T + 511) // 512
        pr = self.pair()
        for kc in range(KC):
            sq = self.sq[kc % 2]
            sk = ("sq", kc % 2)
            P.op("act", lambda e, sq=sq, kc=kc: e.activation(out=sq[:, :T], in_=X[:, kc, :T], func=AF.Square),
                 [xk(kc)], [sk])
            for b in range(nb):
                w = min(512, T - b * 512)
                P.op("pe", lambda e, sq=sq, b=b, kc=kc, w=w: e.matmul(
                    self.psb[pr[b]][:, :w], lhsT=self.ones[:], rhs=sq[:, b * 512:b * 512 + w],
                    start=(kc == 0), stop=(kc == KC - 1)), [sk, "ones"], [("ps", pr[b])])
        for b in range(nb):
            w = min(512, T - b * 512)
            P.op("act", lambda e, b=b, w=w: e.activation(out=self.rstd[:, b * 512:b * 512 + w], in_=self.psb[pr[b]][:, :w],
                                                        func=AF.Sqrt, bias=EPS, scale=1.0 / nfeat),
                 [("ps", pr[b])], [("rstd", b)])
            P.op("dve", lambda e, b=b, w=w: e.reciprocal(out=self.rstd[:, b * 512:b * 512 + w], in_=self.rstd[:, b * 512:b * 512 + w]),
                 [("rstd", b)], [("rstd", b)])
        rk = [("rstd", b) for b in range(nb)]
        for kc in range(KC):
            oap = H[:, kc, :T] if hout is None else hout(kc)
            P.op("dve", lambda e, kc=kc, oap=oap: e.scalar_tensor_tensor(
                out=oap, in0=X[:, kc, :T], scalar=gcol[:, kc:kc + 1], in1=self.rstd[:, :T],
                op0=ALU.mult, op1=ALU.mult), [xk(kc), gkey] + rk, [hk(kc)])
            if after is not None:
                after(kc)

    def frac_turns(self, out, okey, y, ykey, tmp, tkey, n):
        P = self.P
        MAGIC = 12582912.0
        tw = tmp.shape[-1]
        for c0 in range(0, n, tw):
            c1 = min(n, c0 + tw)
            w = c1 - c0
            P.op("dve", lambda e, c0=c0, c1=c1, w=w: e.tensor_scalar(out=tmp[:, 0:w], in0=y[:, c0:c1], scalar1=MAGIC, scalar2=None, op0=ALU.add),
                 [ykey], [tkey])
            P.op("dve", lambda e, w=w: e.tensor_scalar(out=tmp[:, 0:w], in0=tmp[:, 0:w], scalar1=MAGIC, scalar2=None, op0=ALU.subtract),
                 [tkey], [tkey])
            P.op("dve", lambda e, c0=c0, c1=c1, w=w: e.tensor_tensor(out=out[:, c0:c1], in0=y[:, c0:c1], in1=tmp[:, 0:w], op=ALU.subtract),
                 [ykey, tkey], [okey])

    def lin(self, pr, W, wkey, n0, m, X, xk, KC, T):
        nb = (T + 511) // 512
        for kc in range(KC):
            for b in range(nb):
                w = min(512, T - b * 512)
                self.P.op("pe", lambda e, kc=kc, b=b, w=w: e.matmul(
                    self.psb[pr[b]][0:m, :w], lhsT=W[:, kc, n0:n0 + m], rhs=X[:, kc, b * 512:b * 512 + w],
                    start=(kc == 0), stop=(kc == KC - 1)), [wkey, xk(kc)], [("ps", pr[b])])

    def load_w(self, dst, dram, KC, n0, n, key, eng="pool"):
        P = self.P
        step = max(1, 4096 // n)
        for k0 in range(0, KC, step):
            k1 = min(KC, k0 + step)
            P.dma(eng, dst[:, k0:k1, 0:n], dram[k0 * 128:k1 * 128, n0:n0 + n].rearrange("(kc p) n -> p kc n", p=128),
                  writes=[key], key=key)


def build_A(P=None, io=None, NS=2):
    P, io, own = _own(P, io)
    B = Builder(P, 1024)
    T = 1024
    xT = io("xT", [D, NS * 1024], F32, "ExternalInput")
    wa = io("wa", [D, NA], F32, "ExternalInput")
    gm = io("gm", [128, 8], F32, "ExternalInput")
    out = io("projT", [NA, NS * 1024], F32, "ExternalOutput")
    itok = io.over.get("itok")
    if itok is not None:
        wi_sb = P.sb([128, 8, 512], BF16, name="wi_sb")
        B.load_w(wi_sb, wa, 8, 2336, 512, "wi_sb")
        sti = [P.sb([128, 512], F32, name=f"sti{i}") for i in range(2)]
    X = P.sb([128, 8, T], F32, name="X")
    H = P.sb([128, 8, T], BF16, name="H")
    g_sb = P.sb([128, 8], F32, name="g")
    wb = [P.sb([128, 8, 512], BF16, name=f"wb{i}") for i in range(3)]
    st = [P.sb([128, T], F32, name=f"st{i}") for i in range(4)]
    P.dma("sp", g_sb[:], gm, writes=["g"], key="g")
    si = 0
    wi = 0
    for s in range(NS):
        for kc in range(8):
            P.dma("sp", X[:, kc, :], xT[kc * 128:(kc + 1) * 128, s * T:(s + 1) * T], writes=[("X", kc)], key=("X", kc))
        B.rmsnorm(X, lambda kc: ("X", kc), 8, T, g_sb, "g", H, lambda kc: ("H", kc), D)
        if itok is not None:
            for tb in range(8):
                bk = B.bank()
                for kc in range(8):
                    P.op("pe", lambda e, kc=kc, tb=tb, bk=bk: e.matmul(B.psb[bk][:, :], lhsT=H[:, kc, tb * 128:(tb + 1) * 128], rhs=wi_sb[:, kc, :],
                                                                      start=(kc == 0), stop=(kc == 7)), [("H", kc), "wi_sb"], [("ps", bk)])
                stt = sti[tb % 2]
                B.copy(B.ev_eng(), stt[:, :], B.psb[bk][:, :], [("ps", bk)], [("sti", tb % 2)])
                r0 = s * T + tb * 128
                P.dma("sp", itok[r0:r0 + 128, :], stt[:, :], reads=[("sti", tb % 2)], key=("sti", tb % 2))
        for c0 in range(0, NA, 512):
            n = min(512, NA - c0)
            w = wb[wi % 3]
            wk = ("wb", wi % 3)
            wi += 1
            B.load_w(w, wa, 8, c0, n, wk)
            for n0 in range(0, n, 128):
                m = min(128, n - n0)
                pr = B.pair()
                B.lin(pr, w, wk, n0, m, H, lambda kc: ("H", kc), 8, T)
                stt = st[si % 4]
                sk = ("st", si % 4)
                si += 1
                for b in range(2):
                    B.copy(B.ev_eng(), stt[0:m, b * 512:(b + 1) * 512], B.psb[pr[b]][0:m, :], [("ps", pr[b])], [sk])
                P.dma("sp", out[c0 + n0:c0 + n0 + m, s * T:(s + 1) * T], stt[0:m, :], reads=[sk], key=sk)
                if c0 + n0 == 1280 and io.over.get("krs") is not None:
                    krs = io.over["krs"]
                    P.dma("sp", krs[0:16, s * T:(s + 1) * T], stt[16:32, :], reads=[sk], key=sk)
                    P.dma("sp", krs[16:32, s * T:(s + 1) * T], stt[0:16, :], reads=[sk], key=sk)
    P.end()
    if own:
        P.finish()
    return P.nc


def build_C(final, P=None, io=None, NS=2):
    P, io, own = _own(P, io)
    B = Builder(P, 1024)
    T = 1024
    xT = io("xT", [D, NS * 1024], F32, "ExternalInput")
    ysT = io("ysT", [512, NS * 1024], BF16, "ExternalInput")
    omT = io("omT", [512, NS * 1024], BF16, "ExternalInput")
    ohT = io("ohT", [512, NS * 1024], BF16, "ExternalInput")
    memT = io("memT", [D, 256], F32, "ExternalInput")
    gv = io("gv", [128, 5, 8], F32, "ExternalInput")
    w_gate = io("w_gate", [D, 3 * D], F32, "ExternalInput")
    w_glu = io("w_glu", [512, 2 * D], F32, "ExternalInput")
    w_mo = io("w_mo", [512, D], F32, "ExternalInput")
    w_ho = io("w_ho", [512, D], F32, "ExternalInput")
    w_out = io("w_out", [D, D], F32, "ExternalInput")
    w_xq = io("w_xq", [D, 512], F32, "ExternalInput")
    w_xkv = io("w_xkv", [D, D], F32, "ExternalInput")
    w_xo = io("w_xo", [512, D], F32, "ExternalInput")
    w_gu = io("w_gu", [D, 2 * DFF], F32, "ExternalInput")
    w_dn = io("w_dn", [DFF, D], F32, "ExternalInput")
    out = io("xoT", [D, NS * 1024], F32, "ExternalOutput")

    X = P.sb([128, 8, T], F32, name="X")
    ab = [P.sb([128, 8, T], BF16, name=f"ab{i}") for i in range(4)]
    wb = [P.sb([128, 8192], BF16, name=f"wb{i}") for i in range(3)]
    g_sb = P.sb([128, 5, 8], F32, name="g")
    Gt = [P.sb([128, T], BF16, name=f"G{i}") for i in range(2)]
    sc = [P.sb([128, T], F32, name=f"sc{i}") for i in range(2)]
    E = [P.sb([128, 512], BF16, name=f"E{i}") for i in range(2)]
    rc = P.sb([128, 512], F32, name="rc")
    KT = P.sb([128, 4, 256], BF16, name="KT")
    Vm = P.sb([128, 2, 512], BF16, name="Vm")
    Xm = P.sb([128, 8, 256], F32, name="Xm")
    P.dma("sp", g_sb[:], gv, writes=["g"], key="g")
    wi = [0]

    def wnext():
        i = wi[0] % 3
        wi[0] += 1
        return wb[i], ("wb", i)

    def ak(i):
        return lambda kc: ("ab", i, kc)

    H, hk = ab[0], ak(0)

    for kc in range(8):
        P.dma("sp", Xm[:, kc, :], memT[kc * 128:(kc + 1) * 128, :], writes=[("Xm", kc)], key=("Xm", kc))
    B.rmsnorm(Xm, lambda kc: ("Xm", kc), 8, 256, g_sb[:, 2, :], "g", H, hk, D)
    w, wk = wnext()
    wv = w[:, :].rearrange("p (k n) -> p k n", k=8)
    B.load_w(wv, w_xkv, 8, 0, 1024, wk)
    for hh in range(4):
        bk = B.bank()
        B.lin([bk], wv, wk, hh * 128, 128, H, hk, 8, 256)
        B.copy(B.ev_eng(), KT[:, hh, :], B.psb[bk][:, 0:256], [("ps", bk)], [("KT", hh)])
    for mb in range(2):
        bk = B.bank()
        for kc in range(8):
            P.op("pe", lambda e, kc=kc, mb=mb, bk=bk, wv=wv: e.matmul(
                B.psb[bk][:, :], lhsT=H[:, kc, mb * 128:(mb + 1) * 128], rhs=wv[:, kc, 512:1024],
                start=(kc == 0), stop=(kc == 7)), [hk(kc), wk], [("ps", bk)])
        B.copy(B.ev_eng(), Vm[:, mb, :], B.psb[bk][:, :], [("ps", bk)], [("Vm", mb)])

    def resid_add(pr, n):
        for b in range(2):
            P.op("dve", lambda e, b=b, n=n: e.tensor_tensor(out=X[:, n, b * 512:(b + 1) * 512], in0=X[:, n, b * 512:(b + 1) * 512],
                                                            in1=B.psb[pr[b]][:, :], op=ALU.add),
                 [("X", n), ("ps", pr[b])], [("X", n)])

    for s in range(NS):
        tok = slice(s * T, (s + 1) * T)
        for kc in range(8):
            P.dma("sp", X[:, kc, :], xT[kc * 128:(kc + 1) * 128, tok], writes=[("X", kc)], key=("X", kc))
        for kc in range(4):
            P.dma("sp", ab[1][:, kc, :], ysT[kc * 128:(kc + 1) * 128, tok], writes=[("ab", 1, kc)], key=("ab", 1, kc))
            P.dma("sp", ab[1][:, 4 + kc, :], omT[kc * 128:(kc + 1) * 128, tok], writes=[("ab", 1, 4 + kc)], key=("ab", 1, 4 + kc))
            P.dma("sp", ab[2][:, kc, :], ohT[kc * 128:(kc + 1) * 128, tok], writes=[("ab", 2, kc)], key=("ab", 2, kc))
        B.rmsnorm(X, lambda kc: ("X", kc), 8, T, g_sb[:, 0, :], "g", H, hk, D)
        M = ab[3]
        for br in range(3):
            wg, wgk = wnext()
            wgv = wg[:, :].rearrange("p (k n) -> p k n", k=8)
            B.load_w(wgv, w_gate, 8, br * D, D, wgk)
            wy, wyk = wnext()
            if br == 0:
                wyv = wy[:, :].rearrange("p (k n) -> p k n", k=4)
                B.load_w(wyv, w_glu, 4, 0, 2048, wyk)
                Yin, yk = ab[1], (lambda kc: ("ab", 1, kc))
            else:
                wyv = wy[:, 0:4096].rearrange("p (k n) -> p k n", k=4)
                B.load_w(wyv, w_mo if br == 1 else w_ho, 4, 0, 1024, wyk)
                if br == 1:
                    Yin, yk = ab[1][:, 4:8, :], (lambda kc: ("ab", 1, 4 + kc))
                else:
                    Yin, yk = ab[2], (lambda kc: ("ab", 2, kc))
            for j in range(8):
                pg = B.pair()
                B.lin(pg, wgv, wgk, j * 128, 128, H, hk, 8, T)
                G = Gt[j % 2]
                gk = ("G", j % 2)
                for b in range(2):
                    P.op("act", lambda e, b=b, G=G, pg=pg: e.activation(out=G[:, b * 512:(b + 1) * 512], in_=B.psb[pg[b]][:, :], func=AF.Sigmoid),
                         [("ps", pg[b])], [gk])
                py = B.pair()
                B.lin(py, wyv, wyk, j * 128, 128, Yin, yk, 4, T)
                s0 = sc[0]
                if br == 0:
                    pz = B.pair()
                    B.lin(pz, wyv, wyk, D + j * 128, 128, Yin, yk, 4, T)
                    for b in range(2):
                        bs = slice(b * 512, (b + 1) * 512)
                        P.op("act", lambda e, bs=bs, b=b, pz=pz: e.activation(out=sc[1][:, bs], in_=B.psb[pz[b]][:, :], func=AF.Sigmoid),
                             [("ps", pz[b])], [("sc", 1)])
                        P.op("dve", lambda e, bs=bs, b=b, py=py: e.tensor_tensor(out=s0[:, bs], in0=sc[1][:, bs], in1=B.psb[py[b]][:, :], op=ALU.mult),
                             [("sc", 1), ("ps", py[b])], [("sc", 0)])
                        P.op("dve", lambda e, bs=bs, G=G, j=j: e.tensor_tensor(out=M[:, j, bs], in0=s0[:, bs], in1=G[:, bs], op=ALU.mult),
                             [("sc", 0), gk], [("ab", 3, j)])
                else:
                    for b in range(2):
                        bs = slice(b * 512, (b + 1) * 512)
                        P.op("dve", lambda e, bs=bs, b=b, py=py, G=G: e.tensor_tensor(out=s0[:, bs], in0=G[:, bs], in1=B.psb[py[b]][:, :], op=ALU.mult),
                             [gk, ("ps", py[b])], [("sc", 0)])
                        P.op("dve", lambda e, bs=bs, j=j: e.tensor_tensor(out=M[:, j, bs], in0=M[:, j, bs], in1=s0[:, bs], op=ALU.add),
                             [("sc", 0), ("ab", 3, j)], [("ab", 3, j)])
        w, wk = wnext()
        wv = w[:, :].rearrange("p (k n) -> p k n", k=8)
        B.load_w(wv, w_out, 8, 0, D, wk)
        for n in range(8):
            pr = B.pair()
            B.lin(pr, wv, wk, n * 128, 128, M, ak(3), 8, T)
            resid_add(pr, n)
        B.rmsnorm(X, lambda kc: ("X", kc), 8, T, g_sb[:, 1, :], "g", H, hk, D)
        w, wk = wnext()
        wv = w[:, 0:4096].rearrange("p (k n) -> p k n", k=8)
        B.load_w(wv, w_xq, 8, 0, 512, wk)
        Q, qk = ab[1], ak(1)
        O = ab[2]
        for hh in range(4):
            pr = B.pair()
            B.lin(pr, wv, wk, hh * 128, 128, H, hk, 8, T)
            for b in range(2):
                P.op("act", lambda e, b=b, hh=hh, pr=pr: e.activation(out=Q[:, hh, b * 512:(b + 1) * 512], in_=B.psb[pr[b]][:, :],
                                                                 func=AF.Copy, scale=1.0 / math.sqrt(128.0)),
                     [("ps", pr[b])], [qk(hh)])
        for hh in range(4):
            for b in range(2):
                bs = slice(b * 512, (b + 1) * 512)
                po = B.pair()
                for mb in range(2):
                    bk = B.bank()
                    P.op("pe", lambda e, hh=hh, mb=mb, bs=bs, bk=bk: e.matmul(B.psb[bk][:, :], lhsT=KT[:, hh, mb * 128:(mb + 1) * 128],
                                                                            rhs=Q[:, hh, bs], start=True, stop=True),
                         [("KT", hh), qk(hh)], [("ps", bk)])
                    Et = E[mb]
                    P.op("act", lambda e, Et=Et, bk=bk: e.activation(out=Et[:, :], in_=B.psb[bk][:, :], func=AF.Exp),
                         [("ps", bk)], [("E", mb)])
                    P.op("pe", lambda e, hh=hh, mb=mb, Et=Et, po=po: e.matmul(B.psb[po[0]][:, :], lhsT=Vm[:, mb, hh * 128:(hh + 1) * 128],
                                                                          rhs=Et[:, :], start=(mb == 0), stop=(mb == 1)),
                         [("Vm", mb), ("E", mb)], [("ps", po[0])])
                    P.op("pe", lambda e, mb=mb, Et=Et, po=po: e.matmul(B.psb[po[1]][:, :], lhsT=B.ones[:], rhs=Et[:, :],
                                                                   start=(mb == 0), stop=(mb == 1)),
                         ["ones", ("E", mb)], [("ps", po[1])])
                P.op("dve", lambda e, po=po: e.reciprocal(out=rc[:, :], in_=B.psb[po[1]][:, :]), [("ps", po[1])], ["rc"])
                P.op("dve", lambda e, po=po, hh=hh, bs=bs: e.tensor_tensor(out=O[:, hh, bs], in0=rc[:, :], in1=B.psb[po[0]][:, :], op=ALU.mult),
                     ["rc", ("ps", po[0])], [("ab", 2, hh)])
        w, wk = wnext()
        wv = w[:, 0:4096].rearrange("p (k n) -> p k n", k=4)
        B.load_w(wv, w_xo, 4, 0, D, wk)
        for n in range(8):
            pr = B.pair()
            B.lin(pr, wv, wk, n * 128, 128, O, ak(2), 4, T)
            resid_add(pr, n)
        B.rmsnorm(X, lambda kc: ("X", kc), 8, T, g_sb[:, 3, :], "g", H, hk, D)
        for hf in range(2):
            ci = 0
            for blk in (4, 4, 3):
                w, wk = wnext()
                wgv = w[:, 0:4096].rearrange("p (k n) -> p k n", k=8)
                wuv = w[:, 4096:8192].rearrange("p (k n) -> p k n", k=8)
                c0 = (hf * 11 + ci) * 128
                B.load_w(wgv, w_gu, 8, c0, blk * 128, wk)
                B.load_w(wuv, w_gu, 8, DFF + c0, blk * 128, wk)
                for q in range(blk):
                    i = ci + q
                    A, akk = (ab[1], ("ab", 1, i)) if i < 8 else (ab[2], ("ab", 2, i - 8))
                    ai = i if i < 8 else i - 8
                    pg = B.pair()
                    B.lin(pg, wgv, wk, q * 128, 128, H, hk, 8, T)
                    pu = B.pair()
                    B.lin(pu, wuv, wk, q * 128, 128, H, hk, 8, T)
                    for b in range(2):
                        bs = slice(b * 512, (b + 1) * 512)
                        P.op("act", lambda e, bs=bs, b=b, pg=pg: e.activation(out=sc[1][:, bs], in_=B.psb[pg[b]][:, :], func=AF.Silu),
                             [("ps", pg[b])], [("sc", 1)])
                        P.op("dve", lambda e, bs=bs, b=b, pu=pu, A=A, ai=ai: e.tensor_tensor(out=A[:, ai, bs], in0=sc[1][:, bs], in1=B.psb[pu[b]][:, :], op=ALU.mult),
                             [("sc", 1), ("ps", pu[b])], [akk])
                ci += blk
            for nb2 in range(2):
                w, wk = wnext()
                wv = w[:, 0:11 * 512].rearrange("p (k n) -> p k n", k=11)
                P.dma("pool", wv[:, :, :], w_dn[hf * 1408:(hf + 1) * 1408, nb2 * 512:(nb2 + 1) * 512].rearrange("(kc p) n -> p kc n", p=128),
                      writes=[wk], key=wk)
                for n4 in range(4):
                    n = nb2 * 4 + n4
                    pr = B.pair()
                    for i in range(11):
                        A, akk = (ab[1], ("ab", 1, i)) if i < 8 else (ab[2], ("ab", 2, i - 8))
                        ai = i if i < 8 else i - 8
                        for b in range(2):
                            P.op("pe", lambda e, i=i, b=b, A=A, ai=ai, n4=n4, pr=pr, wv=wv: e.matmul(
                                B.psb[pr[b]][:, :], lhsT=wv[:, i, n4 * 128:(n4 + 1) * 128], rhs=A[:, ai, b * 512:(b + 1) * 512],
                                start=(i == 0), stop=(i == 10)), [wk, akk], [("ps", pr[b])])
                    resid_add(pr, n)
        if final:
            def fin_after(kc, tok=tok):
                P.dma("sp", out[kc * 128:(kc + 1) * 128, tok], sc[kc % 2][:, :], reads=[("sc", kc % 2)], key=("sc", kc % 2))
            B.rmsnorm(X, lambda kc: ("X", kc), 8, T, g_sb[:, 4, :], "g", None, lambda kc: ("sc", kc % 2), D,
                      hout=lambda kc: sc[kc % 2][:, :], after=fin_after)
        else:
            for kc in range(8):
                P.dma("sp", out[kc * 128:(kc + 1) * 128, tok], X[:, kc, :], reads=[("X", kc)], key=("X", kc))
    P.end()
    if own:
        P.finish()
    return P.nc


S = 4096
PI = math.pi
TWO_PI = 2 * math.pi * (1.0 - 1e-6)


def build_MLA(P=None, io=None):
    P, io, own = _own(P, io)
    B = Builder(P, 1024)
    T = 1024
    qlatT = io("qlatT", [512, S], F32, "ExternalInput")
    kvlatT = io("kvlatT", [256, S], F32, "ExternalInput")
    krT = io("krT", [32, S], F32, "ExternalInput")
    krsT = io("krsT", [32, S], F32, "ExternalInput")
    pos = io("pos", [32, S], F32, "ExternalInput")
    cst = io("cst", [32, 2], F32, "ExternalInput")
    tri = io("tri", [128, 128], F32, "ExternalInput")
    gq = io("gq", [128, 4], F32, "ExternalInput")
    gkv = io("gkv", [128, 2], F32, "ExternalInput")
    wq = io("wq", [512, 384], F32, "ExternalInput")
    wqs = io("wqs", [512, 384], F32, "ExternalInput")
    wkn = io("wkn", [256, 256], F32, "ExternalInput")
    wv = io("wv", [256, 256], F32, "ExternalInput")
    out = io("omT", [256, S], BF16, "ExternalOutput")

    cosT = P.sb([96, S], F32, name="cosT")
    sinT = P.sb([96, S], F32, name="sinT")
    c_sb = P.sb([96, 2], F32, name="c_sb")
    Q = [P.sb([96, S], BF16, name=f"Q{h}") for h in range(4)]
    Kt = [P.sb([96, S], BF16, name=f"K{h}") for h in range(4)]
    Vt = P.sb([128, 32, 4, 128], BF16, name="Vt")
    tri_f = P.sb([128, 128], F32, name="tri_f")
    tri_b = P.sb([128, 128], BF16, name="tri_b")
    gq_sb = P.sb([128, 4], F32, name="gq_s")
    gkv_sb = P.sb([128, 2], F32, name="gkv_s")
    X = P.sb([128, 4, T], F32, name="X")
    H = P.sb([128, 4, T], BF16, name="H")
    KR = P.sb([96, 2, T], F32, name="KR")
    tm = [P.sb([96, T], F32, name=f"tm{i}") for i in range(2)]
    wq_sb = P.sb([128, 4, 384], BF16, name="wq_sb")
    wqs_sb = P.sb([128, 4, 384], BF16, name="wqs_sb")
    wkn_sb = P.sb([128, 2, 256], BF16, name="wkn_sb")
    wv_sb = P.sb([128, 2, 256], BF16, name="wv_sb")
    E = [P.sb([128, 512], BF16, name=f"E{i}") for i in range(3)]
    Osb = [P.sb([128, 512], F32, name=f"Osb{i}") for i in range(2)]
    sums = [P.sb([64, 512], F32, name=f"sums{i}") for i in range(2)]
    ost = [P.sb([64, 512], BF16, name=f"ost{i}") for i in range(2)]

    P.dma("sp", cosT[64:96, :], pos, writes=["cos"], key="cos")
    P.dma("sp", c_sb[64:96, :], cst, writes=["c"], key="c")
    P.dma("sp", tri_f[:], tri, writes=["trif"], key="trif")
    P.dma("sp", gq_sb[:], gq, writes=["gq"], key="gq")
    P.dma("sp", gkv_sb[:], gkv, writes=["gkv"], key="gkv")
    B.load_w(wq_sb, wq, 4, 0, 384, "wq")
    B.load_w(wqs_sb, wqs, 4, 0, 384, "wqs")
    B.load_w(wkn_sb, wkn, 2, 0, 256, "wkn")
    B.load_w(wv_sb, wv, 2, 0, 256, "wv")
    P.op("dve", lambda e: e.tensor_copy(out=tri_b[:], in_=tri_f[:]), ["trif"], ["tri"])
    P.op("pool", lambda e: e.memset(Vt[:, :, :, 64:128], 1.0), [], ["Vones"])
    r = slice(64, 96)
    P.op("dve", lambda e: e.tensor_scalar(out=cosT[r, :], in0=cosT[r, :], scalar1=c_sb[r, 0:1], scalar2=None, op0=ALU.mult),
         ["cos", "c"], ["cos"])
    B.frac_turns(sinT[r, :], "sin", cosT[r, :], "cos", tm[0][r, :], ("tm", 0), S)
    P.op("dve", lambda e: e.tensor_scalar(out=cosT[r, :], in0=cosT[r, :], scalar1=0.25, scalar2=None, op0=ALU.add), ["cos"], ["cos"])
    B.frac_turns(cosT[r, :], "cos", cosT[r, :], "cos", tm[1][r, :], ("tm", 1), S)
    P.op("act", lambda e: e.activation(out=sinT[r, :], in_=sinT[r, :], func=AF.Sin, scale=TWO_PI), ["sin"], ["sin"])
    P.op("act", lambda e: e.activation(out=cosT[r, :], in_=cosT[r, :], func=AF.Sin, scale=TWO_PI), ["cos"], ["cos"])
    P.op("dve", lambda e: e.tensor_scalar(out=sinT[r, :], in0=sinT[r, :], scalar1=c_sb[r, 1:2], scalar2=None, op0=ALU.mult),
         ["sin", "c"], ["sin"])

    for s in range(4):
        tok = slice(s * T, (s + 1) * T)
        for kc in range(4):
            P.dma("sp", X[:, kc, :], qlatT[kc * 128:(kc + 1) * 128, tok], writes=[("X", kc)], key=("X", kc))
        B.rmsnorm(X, lambda kc: ("X", kc), 4, T, gq_sb, "gq", H, lambda kc: ("H", kc), 512)
        for h in range(4):
            pa = B.pair()
            B.lin(pa, wq_sb, "wq", h * 96, 96, H, lambda kc: ("H", kc), 4, T)
            pb = B.pair()
            B.lin(pb, wqs_sb, "wqs", h * 96, 96, H, lambda kc: ("H", kc), 4, T)
            for b in range(2):
                bs = slice(b * 512, (b + 1) * 512)
                gs = slice(s * T + b * 512, s * T + (b + 1) * 512)
                P.op("act", lambda e, h=h, gs=gs, pa=pa, b=b: e.activation(out=Q[h][0:64, gs], in_=B.psb[pa[b]][0:64, :], func=AF.Copy),
                     [("ps", pa[b])], [("Q", h, s)])
                P.op("dve", lambda e, bs=bs, gs=gs, pa=pa, b=b: e.tensor_tensor(out=tm[0][r, bs], in0=cosT[r, gs], in1=B.psb[pa[b]][r, :], op=ALU.mult),
                     ["cos", ("ps", pa[b])], [("tm", 0)])
                P.op("dve", lambda e, bs=bs, gs=gs, pb=pb, b=b: e.tensor_tensor(out=tm[1][r, bs], in0=sinT[r, gs], in1=B.psb[pb[b]][r, :], op=ALU.mult),
                     ["sin", ("ps", pb[b])], [("tm", 1)])
                P.op("dve", lambda e, h=h, bs=bs, gs=gs: e.tensor_tensor(out=Q[h][r, gs], in0=tm[0][r, bs], in1=tm[1][r, bs], op=ALU.add),
                     [("tm", 0), ("tm", 1)], [("Q", h, s)])
        for kc in range(2):
            P.dma("sp", X[:, kc, :], kvlatT[kc * 128:(kc + 1) * 128, tok], writes=[("X", kc)], key=("X", kc))
        P.dma("sp", KR[r, 0, :], krT[:, tok], writes=["KR0"], key="KR0")
        P.dma("sp", KR[r, 1, :], krsT[:, tok], writes=["KR1"], key="KR1")
        B.rmsnorm(X, lambda kc: ("X", kc), 2, T, gkv_sb, "gkv", H, lambda kc: ("H", kc), 256)
        gsl = slice(s * T, (s + 1) * T)
        P.op("dve", lambda e, gsl=gsl: e.tensor_tensor(out=tm[0][r, :], in0=KR[r, 0, :], in1=cosT[r, gsl], op=ALU.mult),
             ["KR0", "cos"], [("tm", 0)])
        P.op("dve", lambda e, gsl=gsl: e.tensor_tensor(out=tm[1][r, :], in0=KR[r, 1, :], in1=sinT[r, gsl], op=ALU.mult),
             ["KR1", "sin"], [("tm", 1)])
        for h in range(4):
            P.op("pool", lambda e, h=h, gsl=gsl: e.tensor_tensor(out=Kt[h][r, gsl], in0=tm[0][r, :], in1=tm[1][r, :], op=ALU.add),
                 [("tm", 0), ("tm", 1)], [("K", h, s)])
            pa = B.pair()
            B.lin(pa, wkn_sb, "wkn", h * 64, 64, H, lambda kc: ("H", kc), 2, T)
            for b in range(2):
                gs = slice(s * T + b * 512, s * T + (b + 1) * 512)
                B.copy(B.ev_eng(), Kt[h][0:64, gs], B.psb[pa[b]][0:64, :], [("ps", pa[b])], [("K", h, s)])
        for tb in range(8):
            blk = s * 8 + tb
            bk = B.bank()
            for kc in range(2):
                P.op("pe", lambda e, kc=kc, tb=tb, bk=bk: e.matmul(B.psb[bk][:, 0:256], lhsT=H[:, kc, tb * 128:(tb + 1) * 128], rhs=wv_sb[:, kc, :],
                                                                  start=(kc == 0), stop=(kc == 1)), [("H", kc), "wv"], [("ps", bk)])
            B.copy(B.ev_eng(), Vt[:, blk, :, 0:64], B.psb[bk][:, 0:256].rearrange("p (h d) -> p h d", h=4), [("ps", bk)], [("V", blk)])

    scale = 1.0 / math.sqrt(96.0)
    sti = 0
    ei = 0
    for h in range(4):
        for qr in range(8):
            it = h * 8 + qr
            acc = 6 + it % 2
            nkb = 4 * qr + 4
            for kb in range(nkb):
                j = kb - 4 * qr
                c0 = 128 * j if j > 0 else 0
                bk = sti % 6
                sti += 1
                Et = E[ei % 3]
                ek = ("E", ei % 3)
                ei += 1
                ksl = slice(kb * 128, (kb + 1) * 128)
                qsl = slice(qr * 512 + c0, qr * 512 + 512)
                P.op("pe", lambda e, h=h, bk=bk, c0=c0, ksl=ksl, qsl=qsl: e.matmul(B.psb[bk][:, c0:512], lhsT=Kt[h][:, ksl], rhs=Q[h][:, qsl],
                                                                               start=True, stop=True),
                     [("K", h, kb // 8), ("Q", h, qr // 2)], [("ps", bk)])
                P.op("act", lambda e, bk=bk, c0=c0, Et=Et: e.activation(out=Et[:, c0:512], in_=B.psb[bk][:, c0:512], func=AF.Exp, scale=scale),
                     [("ps", bk)], [ek])
                if j >= 0:
                    P.op("dve", lambda e, c0=c0, Et=Et: e.tensor_tensor(out=Et[:, c0:c0 + 128], in0=Et[:, c0:c0 + 128], in1=tri_b[:, :], op=ALU.mult),
                         [ek, "tri"], [ek])
                P.op("pe", lambda e, h=h, kb=kb, c0=c0, Et=Et, acc=acc, nkb=nkb: e.matmul(B.psb[acc][:, c0:512], lhsT=Vt[:, kb, h, :], rhs=Et[:, c0:512],
                                                                                   start=(kb == 0), stop=(kb == nkb - 1)),
                     [("V", kb), "Vones", ek], [("ps", acc)])
            Ob = Osb[it % 2]
            ok = ("Osb", it % 2)
            sm = sums[it % 2]
            smk = ("sums", it % 2)
            os_ = ost[it % 2]
            osk = ("ost", it % 2)
            P.op("act", lambda e, Ob=Ob, acc=acc: e.activation(out=Ob[:, :], in_=B.psb[acc][:, :], func=AF.Copy), [("ps", acc)], [ok])
            P.dma("sp", sm[:, :], Ob[64:128, :], reads=[ok], writes=[smk], key=smk)
            P.op("dve", lambda e, sm=sm: e.reciprocal(out=sm[:, :], in_=sm[:, :]), [smk], [smk])
            P.op("dve", lambda e, sm=sm, Ob=Ob, os_=os_: e.tensor_tensor(out=os_[:, :], in0=Ob[0:64, :], in1=sm[:, :], op=ALU.mult), [smk, ok], [osk])
            P.dma("sp", out[h * 64:(h + 1) * 64, qr * 512:(qr + 1) * 512], os_[:, :], reads=[osk], key=osk)
    P.end()
    if own:
        P.finish()
    return P.nc


def build_HG(layer, P=None, io=None):
    P, io, own = _own(P, io)
    B = Builder(P, 1024)
    T = 1024
    qT = io("qT", [256, S], F32, "ExternalInput")
    fT = io("fT", [256, S], F32, "ExternalInput")
    itok = io("itok", [S, 256], F32, "ExternalInput")
    gT = io("gT", [256, S], F32, "ExternalInput")
    lbp = io("lbp", [128, 2, 2], F32, "ExternalInput")
    gn = io("gn", [128, 1], F32, "ExternalInput")
    cm = io("cm", [64, 512], F32, "ExternalInput")
    idn = io("idn", [128, 128], F32, "ExternalInput")
    out = io("ohT", [256, S], BF16, "ExternalOutput")

    Ft = P.sb([128, S], F32, name="Ft")
    Kf = P.sb([128, S], F32, name="Kf")
    Bc = P.sb([128, S], F32, name="Bc")
    EB = P.sb([128, S], F32, name="EB")
    msk = P.sb([128, S], F32, name="msk")
    Qt = P.sb([128, S], BF16, name="Qt")
    Kt = P.sb([128, S], BF16, name="Kt")
    Gs = P.sb([128, S], BF16, name="Gs")
    Vtok = P.sb([64, 64, 128], BF16, name="Vtok")
    Oall = P.sb([128, 1, S], F32, name="Oall")
    ebl = P.sb([128, 64], F32, name="ebl")
    lb_sb = P.sb([128, 2, 2], F32, name="lb_sb")
    lbv = P.sb([128, 2], F32, name="lbv")
    oml = P.sb([128, 2], F32, name="oml")
    gn_sb = P.sb([128, 1], F32, name="gn_sb")
    cm_b = P.sb([64, 512], BF16, name="cm_b")
    id_b = P.sb([128, 128], BF16, name="id_b")
    AT = [P.sb([64, 512], BF16, name=f"AT{i}") for i in range(2)]
    Ktok = [P.sb([64, 8, 128], BF16, name=f"Ktok{i}") for i in range(2)]
    Sf = P.sb([128, 128], F32, name="Sf")
    Sb = [P.sb([128, 128], BF16, name=f"Sb{i}") for i in range(4)]
    Hh = P.sb([128, 1, T], BF16, name="Hh")
    ost = [P.sb([128, T], BF16, name=f"ost{i}") for i in range(2)]

    P.dma("sp", lb_sb[:], lbp, writes=["lb_sb"], key="lb_sb")
    P.dma("sp", gn_sb[:], gn, writes=["gn"], key="gn")
    P.dma("pool", cm_b[:], cm, writes=["cm"], key="cm")
    P.dma("pool", id_b[:], idn, writes=["idn"], key="idn")
    if layer == 0:
        P.op("dve", lambda e: e.memset(lbv[:], 0.0), [], ["lbv"])
    else:
        P.op("dve", lambda e: e.tensor_tensor(out=lbv[:], in0=lb_sb[:, 1, :], in1=lb_sb[:, 0, :], op=ALU.subtract), ["lb_sb"], ["lbv"])
        P.op("act", lambda e: e.activation(out=lbv[:], in_=lbv[:], func=AF.Sigmoid), ["lbv"], ["lbv"])
    P.op("dve", lambda e: e.tensor_scalar(out=oml[:], in0=lbv[:], scalar1=-1.0, scalar2=1.0, op0=ALU.mult, op1=ALU.add), ["lbv"], ["oml"])
    P.op("pool", lambda e: e.memset(msk[:], 1.0), [], ["msk"])
    P.op("pool", lambda e: e.memset(msk[:, :].rearrange("p (c j) -> p c j", j=64)[:, :, 0:1], 0.0), ["msk"], ["msk"])

    for h in range(2):
        rows = slice(h * 128, (h + 1) * 128)
        P.dma("sp", Ft[:], fT[rows, :], writes=["Ft"], key="Ft")
        P.op("act", lambda e: e.activation(out=Ft[:], in_=Ft[:], func=AF.Sigmoid), ["Ft"], ["Ft"])
        P.op("dve", lambda e, h=h: e.tensor_scalar(out=Ft[:], in0=Ft[:], scalar1=oml[:, h:h + 1], scalar2=lbv[:, h:h + 1], op0=ALU.mult, op1=ALU.add),
             ["Ft", "oml", "lbv"], ["Ft"])
        P.op("dve", lambda e: e.tensor_scalar(out=Kf[:], in0=Ft[:], scalar1=-1.0, scalar2=1.0, op0=ALU.mult, op1=ALU.add), ["Ft"], ["Kf"])
        P.op("act", lambda e: e.activation(out=Ft[:], in_=Ft[:], func=AF.Ln), ["Ft"], ["Ft"])
        P.op("dve", lambda e: e.tensor_tensor_scan(out=Bc[:], data0=msk[:], data1=Ft[:], initial=0.0, op0=ALU.mult, op1=ALU.add),
             ["msk", "Ft"], ["Bc"])
        P.op("act", lambda e: e.activation(out=EB[:], in_=Bc[:], func=AF.Exp), ["Bc"], ["EB"])
        P.op("act", lambda e: e.activation(out=Bc[:], in_=Bc[:], func=AF.Exp, scale=-1.0), ["Bc"], ["Bc"])
        P.op("dve", lambda e: e.tensor_copy(out=ebl[:], in_=EB[:, :].rearrange("p (c j) -> p c j", j=64)[:, :, 63]), ["EB"], ["ebl"])
        P.op("dve", lambda e: e.tensor_tensor(out=Kt[:], in0=Kf[:], in1=Bc[:], op=ALU.mult), ["Kf", "Bc"], ["Kt"])
        P.dma("sp", Kf[:], qT[rows, :], reads=[], writes=["Kf"], key="Kf")
        P.op("act", lambda e: e.activation(out=Kf[:], in_=Kf[:], func=AF.Silu), ["Kf"], ["Kf"])
        P.op("dve", lambda e: e.tensor_tensor(out=Qt[:], in0=Kf[:], in1=EB[:], op=ALU.mult), ["Kf", "EB"], ["Qt"])
        P.dma("sp", Ft[:], gT[rows, :], writes=["Ft"], key="Ft")
        P.op("act", lambda e: e.activation(out=Gs[:], in_=Ft[:], func=AF.Silu), ["Ft"], ["Gs"])
        P.dma("pool", Vtok[:], itok[:, rows].rearrange("(c s) e -> s c e", s=64), writes=["Vtok"], key="Vtok")
        P.op("dve", lambda e: e.memset(Sf[:], 0.0), [], ["Sf"])
        P.op("pool", lambda e: e.memset(Sb[0][:], 0.0), [], [("Sb", 0)])
        si = 0
        for g in range(8):
            bA, bT, bO = g % 2, 2 + g % 2, 4 + g % 2
            At = AT[g % 2]
            atk = ("AT", g % 2)
            Kk = Ktok[g % 2]
            kkk = ("Ktok", g % 2)
            for cc in range(8):
                cs = slice((g * 8 + cc) * 64, (g * 8 + cc + 1) * 64)
                P.op("pe", lambda e, cs=cs, cc=cc, bA=bA: e.matmul(B.psb[bA][0:64, cc * 64:(cc + 1) * 64], lhsT=Kt[:, cs], rhs=Qt[:, cs], start=True, stop=True),
                     ["Kt", "Qt"], [("ps", bA)])
            P.op("dve", lambda e, bA=bA, At=At: e.tensor_tensor(out=At[:, :], in0=cm_b[:, :], in1=B.psb[bA][0:64, :], op=ALU.mult),
                 ["cm", ("ps", bA)], [atk])
            pstv = B.psb[bT][:, :].bitcast(BF16)
            for cc in range(8):
                cs = slice((g * 8 + cc) * 64, (g * 8 + cc + 1) * 64)
                P.op("pe", lambda e, cs=cs, cc=cc, pstv=pstv: e.transpose(pstv[0:64, cc * 128:(cc + 1) * 128], Kt[:, cs], id_b[:, :]),
                     ["Kt", "idn"], [("ps", bT)])
            P.op("act", lambda e, pstv=pstv, Kk=Kk: e.activation(out=Kk[:, :, :], in_=pstv[0:64, :].rearrange("p (c d) -> p c d", c=8), func=AF.Copy),
                 [("ps", bT)], [kkk])
            for cc in range(8):
                c = g * 8 + cc
                cs = slice(c * 64, (c + 1) * 64)
                Sbc = Sb[si % 4]
                sbk = ("Sb", si % 4)
                Sbn = Sb[(si + 1) % 4]
                sbnk = ("Sb", (si + 1) % 4)
                si += 1
                bS = 6 + c % 2
                P.op("pe", lambda e, c=c, cc=cc, bO=bO, At=At: e.matmul(B.psb[bO][:, cc * 64:(cc + 1) * 64], lhsT=Vtok[:, c, :], rhs=At[:, cc * 64:(cc + 1) * 64],
                                                                      start=True, stop=False), ["Vtok", atk], [("ps", bO)])
                P.op("pe", lambda e, cs=cs, cc=cc, bO=bO, Sbc=Sbc: e.matmul(B.psb[bO][:, cc * 64:(cc + 1) * 64], lhsT=Sbc[:, :], rhs=Qt[:, cs],
                                                                          start=False, stop=True), [sbk, "Qt"], [("ps", bO)])
                if c < 63:
                    P.op("pe", lambda e, c=c, cc=cc, bS=bS, Kk=Kk: e.matmul(B.psb[bS][:, 0:128], lhsT=Kk[:, cc, :], rhs=Vtok[:, c, :], start=True, stop=True),
                         [kkk, "Vtok"], [("ps", bS)])
                    P.op("dve", lambda e, bS=bS: e.tensor_tensor(out=Sf[:, :], in0=Sf[:, :], in1=B.psb[bS][:, 0:128], op=ALU.add), ["Sf", ("ps", bS)], ["Sf"])
                    P.op("act", lambda e, c=c, Sbn=Sbn: e.activation(out=Sbn[:, :], in_=Sf[:, :], func=AF.Copy, scale=ebl[:, c:c + 1]), ["Sf", "ebl"], [sbnk])
                    P.op("dve", lambda e, c=c: e.tensor_scalar(out=Sf[:, :], in0=Sf[:, :], scalar1=ebl[:, c:c + 1], scalar2=None, op0=ALU.mult), ["Sf", "ebl"], ["Sf"])
            P.op("act", lambda e, g=g, bO=bO: e.activation(out=Oall[:, 0, g * 512:(g + 1) * 512], in_=B.psb[bO][:, :], func=AF.Copy), [("ps", bO)], [("Oall", g // 2)])
        for s in range(4):
            B.rmsnorm(Oall[:, :, s * T:(s + 1) * T], lambda kc, s=s: ("Oall", s), 1, T, gn_sb, "gn", Hh, lambda kc: "Hh", 128)
            o_ = ost[s % 2]
            ok = ("ost", s % 2)
            P.op("dve", lambda e, s=s, o_=o_: e.tensor_tensor(out=o_[:, :], in0=Hh[:, 0, :], in1=Gs[:, s * T:(s + 1) * T], op=ALU.mult), ["Hh", "Gs"], [ok])
            P.dma("sp", out[rows, s * T:(s + 1) * T], o_[:, :], reads=[ok], key=ok)
    P.end()
    if own:
        P.finish()
    return P.nc


def build_S5(P=None, io=None):
    P, io, own = _own(P, io)
    B = Builder(P, 512)
    TT = 512
    uT = io("uT", [256, S], F32, "ExternalInput")
    lamre = io("lamre", [128, 8], F32, "ExternalInput")
    lamim = io("lamim", [128, 8], F32, "ExternalInput")
    lstep = io("lstep", [128, 8], F32, "ExternalInput")
    bre = io("bre", [128, 8, 32], F32, "ExternalInput")
    bim = io("bim", [128, 8, 32], F32, "ExternalInput")
    cre = io("cre", [128, 8, 32], F32, "ExternalInput")
    cim = io("cim", [128, 8, 32], F32, "ExternalInput")
    dsk = io("dsk", [32, 8], F32, "ExternalInput")
    jrow = io("jrow", [128, 512], F32, "ExternalInput")
    idn = io("idn", [128, 128], F32, "ExternalInput")
    out = io("ysT", [256, S], BF16, "ExternalOutput")

    def t8(n):
        return P.sb([128, 8], F32, name=n)
    lre, lim, stp, rr, trn, c1, s1, c5, s5, nr, ni, inv, cfr, cfi, ncfi, tA, tB = [t8(n) for n in
        ("lre", "lim", "stp", "rr", "trn", "c1", "s1", "c5", "s5", "nr", "ni", "inv", "cfr", "cfi", "ncfi", "tA", "tB")]
    b_re = P.sb([128, 8, 32], F32, name="b_re")
    b_im = P.sb([128, 8, 32], F32, name="b_im")
    Bre = P.sb([128, 8, 32], F32, name="Bre")
    Bim = P.sb([128, 8, 32], F32, name="Bim")
    t32 = P.sb([128, 32], F32, name="t32")
    Cre = P.sb([128, 8, 32], BF16, name="Cre")
    Cimf = P.sb([128, 8, 32], F32, name="Cimf")
    Cimn = P.sb([128, 8, 32], BF16, name="Cimn")
    BTre = P.sb([32, 8, 128], BF16, name="BTre")
    BTim = P.sb([32, 8, 128], BF16, name="BTim")
    d_sb = P.sb([32, 8], F32, name="d_sb")
    jr = P.sb([128, 512], F32, name="jr")
    id_f = P.sb([128, 128], F32, name="id_f")
    Ecos = P.sb([128, 8, 512], F32, name="Ecos")
    Esin = P.sb([128, 8, 512], F32, name="Esin")
    Rt = P.sb([128, 8, 512], F32, name="Rt")
    ftmp = P.sb([128, 512], F32, name="ftmp")
    zin = P.sb([128, 8, 2], F32, name="zin")
    Ut = [P.sb([32, 8, TT], BF16, name=f"Ut{i}") for i in range(2)]
    Yo = [P.sb([32, 8, TT], BF16, name=f"Yo{i}") for i in range(2)]
    W = [[P.sb([128, TT], F32, name=f"W{i}_{k}") for k in range(6)] for i in range(2)]
    Xb = [[P.sb([128, TT], BF16, name=f"Xb{i}_{k}") for k in range(2)] for i in range(2)]
    yf = [P.sb([32, TT], F32, name=f"yf{i}") for i in range(2)]

    for t, d_ in ((lre, lamre), (lim, lamim), (stp, lstep)):
        P.dma("sp", t[:], d_, writes=[id(t)], key=id(t))
    P.dma("sp", b_re[:], bre, writes=["b_re"], key="b_re")
    P.dma("sp", b_im[:], bim, writes=["b_im"], key="b_im")
    P.dma("pool", Cre[:], cre, writes=["Cre"], key="Cre")
    P.dma("sp", Cimf[:], cim, writes=["Cimf"], key="Cimf")
    P.dma("sp", d_sb[:], dsk, writes=["d"], key="d")
    P.dma("sp", jr[:], jrow, writes=["jr"], key="jr")
    P.dma("sp", id_f[:], idn, writes=["idf"], key="idf")
    P.op("dve", lambda e: e.tensor_scalar(out=Cimn[:], in0=Cimf[:], scalar1=-1.0, scalar2=None, op0=ALU.mult), ["Cimf"], ["Cimn"])
    P.op("pool", lambda e: e.memset(zin[:], 0.0), [], ["zin"])

    K_ = "prm"

    def dv(fn, extra_r=(), extra_w=()):
        P.op("dve", fn, [K_, id(lre), id(lim), id(stp)] + list(extra_r), [K_] + list(extra_w))

    def ac(fn):
        P.op("act", fn, [K_, id(lre), id(lim), id(stp)], [K_])

    ac(lambda e: e.activation(out=stp[:], in_=stp[:], func=AF.Exp))
    dv(lambda e: e.tensor_tensor(out=tA[:], in0=lre[:], in1=stp[:], op=ALU.mult))
    ac(lambda e: e.activation(out=rr[:], in_=tA[:], func=AF.Exp))
    dv(lambda e: e.tensor_tensor(out=trn[:], in0=lim[:], in1=stp[:], op=ALU.mult))
    dv(lambda e: e.tensor_scalar(out=trn[:], in0=trn[:], scalar1=1.0 / (2 * PI), scalar2=None, op0=ALU.mult))

    def sincos(turn_ap, s_out, c_out):
        B.frac_turns(s_out[:], K_, turn_ap, K_, tB[:], K_, 8)
        dv(lambda e: e.tensor_scalar(out=tA[:], in0=turn_ap, scalar1=0.25, scalar2=None, op0=ALU.add))
        B.frac_turns(c_out[:], K_, tA[:], K_, tB[:], K_, 8)
        ac(lambda e: e.activation(out=s_out[:], in_=s_out[:], func=AF.Sin, scale=TWO_PI))
        ac(lambda e: e.activation(out=c_out[:], in_=c_out[:], func=AF.Sin, scale=TWO_PI))

    sincos(trn[:], s1, c1)
    dv(lambda e: e.tensor_scalar(out=ni[:], in0=trn[:], scalar1=512.0, scalar2=None, op0=ALU.mult))
    sincos(ni[:], s5, c5)
    dv(lambda e: e.tensor_tensor(out=nr[:], in0=rr[:], in1=c1[:], op=ALU.mult))
    dv(lambda e: e.tensor_scalar(out=nr[:], in0=nr[:], scalar1=-1.0, scalar2=None, op0=ALU.add))
    dv(lambda e: e.tensor_tensor(out=ni[:], in0=rr[:], in1=s1[:], op=ALU.mult))
    dv(lambda e: e.tensor_tensor(out=tA[:], in0=lre[:], in1=lre[:], op=ALU.mult))
    dv(lambda e: e.tensor_tensor(out=tB[:], in0=lim[:], in1=lim[:], op=ALU.mult))
    dv(lambda e: e.tensor_tensor(out=tA[:], in0=tA[:], in1=tB[:], op=ALU.add))
    dv(lambda e: e.reciprocal(out=inv[:], in_=tA[:]))
    dv(lambda e: e.tensor_tensor(out=tA[:], in0=nr[:], in1=lre[:], op=ALU.mult))
    dv(lambda e: e.tensor_tensor(out=tB[:], in0=ni[:], in1=lim[:], op=ALU.mult))
    dv(lambda e: e.tensor_tensor(out=tA[:], in0=tA[:], in1=tB[:], op=ALU.add))
    dv(lambda e: e.tensor_tensor(out=cfr[:], in0=tA[:], in1=inv[:], op=ALU.mult))
    dv(lambda e: e.tensor_tensor(out=tA[:], in0=ni[:], in1=lre[:], op=ALU.mult))
    dv(lambda e: e.tensor_tensor(out=tB[:], in0=nr[:], in1=lim[:], op=ALU.mult))
    dv(lambda e: e.tensor_tensor(out=tA[:], in0=tA[:], in1=tB[:], op=ALU.subtract))
    dv(lambda e: e.tensor_tensor(out=cfi[:], in0=tA[:], in1=inv[:], op=ALU.mult))
    dv(lambda e: e.tensor_scalar(out=ncfi[:], in0=cfi[:], scalar1=-1.0, scalar2=None, op0=ALU.mult))
    for a in range(8):
        dv(lambda e, a=a: e.tensor_scalar(out=t32[:], in0=b_re[:, a, :], scalar1=cfr[:, a:a + 1], scalar2=None, op0=ALU.mult), ["b_re"])
        dv(lambda e, a=a: e.scalar_tensor_tensor(out=Bre[:, a, :], in0=b_im[:, a, :], scalar=ncfi[:, a:a + 1], in1=t32[:], op0=ALU.mult, op1=ALU.add), ["b_im"])
        dv(lambda e, a=a: e.tensor_scalar(out=t32[:], in0=b_im[:, a, :], scalar1=cfr[:, a:a + 1], scalar2=None, op0=ALU.mult), ["b_im"])
        dv(lambda e, a=a: e.scalar_tensor_tensor(out=Bim[:, a, :], in0=b_re[:, a, :], scalar=cfi[:, a:a + 1], in1=t32[:], op0=ALU.mult, op1=ALU.add), ["b_re"])
    for a in range(8):
        for src, dst, nm in ((Bre, BTre, "BTre"), (Bim, BTim, "BTim")):
            bk = B.bank()
            P.op("pe", lambda e, a=a, src=src, bk=bk: e.transpose(B.psb[bk][0:32, 0:128], src[:, a, :], id_f[:, :]), [K_, "idf"], [("ps", bk)])
            B.copy("act", dst[:, a, :], B.psb[bk][0:32, 0:128], [("ps", bk)], [nm])
    for a in range(8):
        dv(lambda e, a=a: e.tensor_scalar(out=Esin[:, a, :], in0=jr[:], scalar1=trn[:, a:a + 1], scalar2=None, op0=ALU.mult), ["jr"], [("E", a)])
        P.op("dve", lambda e, a=a: e.tensor_scalar(out=Ecos[:, a, :], in0=Esin[:, a, :], scalar1=0.25, scalar2=None, op0=ALU.add), [("E", a)], [("E", a)])
        B.frac_turns(Esin[:, a, :], ("E", a), Esin[:, a, :], ("E", a), ftmp[:], "ftmp", 512)
        B.frac_turns(Ecos[:, a, :], ("E", a), Ecos[:, a, :], ("E", a), ftmp[:], "ftmp", 512)
        P.op("act", lambda e, a=a: e.activation(out=Esin[:, a, :], in_=Esin[:, a, :], func=AF.Sin, scale=TWO_PI), [("E", a)], [("E", a)])
        P.op("act", lambda e, a=a: e.activation(out=Ecos[:, a, :], in_=Ecos[:, a, :], func=AF.Sin, scale=TWO_PI), [("E", a)], [("E", a)])
        dv(lambda e, a=a: e.tensor_scalar(out=Rt[:, a, :], in0=jr[:], scalar1=0.0, scalar2=rr[:, a:a + 1], op0=ALU.mult, op1=ALU.add), ["jr"], [("R", a)])

    wi = 0
    for ti in range(8):
        tok = slice(ti * TT, (ti + 1) * TT)
        U = Ut[ti % 2]
        uk = ("U", ti % 2)
        Y = Yo[ti % 2]
        yk = ("Y", ti % 2)
        P.dma("pool", U[:], uT[:, tok].rearrange("(a q) t -> q a t", q=32), writes=[uk], key=uk)
        for a in range(8):
            ws = W[wi % 2]
            xb = Xb[wi % 2]
            yfa = yf[wi % 2]
            wk = lambda k, i=wi % 2: ("W", i, k)
            wi += 1
            wre, wim, zre, zim, t1, t2 = ws
            ek, rk = ("E", a), ("R", a)
            cosA, sinA = Ecos[:, a, :], Esin[:, a, :]
            b0, b1 = 2 * (a % 2), 2 * (a % 2) + 1
            P.op("pe", lambda e, a=a, U=U, b0=b0: e.matmul(B.psb[b0][:, :], lhsT=BTre[:, a, :], rhs=U[:, a, :], start=True, stop=True), ["BTre", uk], [("ps", b0)])
            P.op("pe", lambda e, a=a, U=U, b1=b1: e.matmul(B.psb[b1][:, :], lhsT=BTim[:, a, :], rhs=U[:, a, :], start=True, stop=True), ["BTim", uk], [("ps", b1)])
            P.op("dve", lambda e, b0=b0, cosA=cosA, wre=wre: e.tensor_tensor(out=wre[:], in0=cosA, in1=B.psb[b0][:, :], op=ALU.mult), [ek, ("ps", b0)], [wk(0)])
            P.op("dve", lambda e, b1=b1, sinA=sinA, t1=t1: e.tensor_tensor(out=t1[:], in0=sinA, in1=B.psb[b1][:, :], op=ALU.mult), [ek, ("ps", b1)], [wk(4)])
            P.op("pool", lambda e, wre=wre, t1=t1: e.tensor_tensor(out=wre[:], in0=wre[:], in1=t1[:], op=ALU.add), [wk(0), wk(4)], [wk(0)])
            P.op("dve", lambda e, b1=b1, cosA=cosA, wim=wim: e.tensor_tensor(out=wim[:], in0=cosA, in1=B.psb[b1][:, :], op=ALU.mult), [ek, ("ps", b1)], [wk(1)])
            P.op("dve", lambda e, b0=b0, sinA=sinA, t2=t2: e.tensor_tensor(out=t2[:], in0=sinA, in1=B.psb[b0][:, :], op=ALU.mult), [ek, ("ps", b0)], [wk(5)])
            P.op("pool", lambda e, wim=wim, t2=t2: e.tensor_tensor(out=wim[:], in0=wim[:], in1=t2[:], op=ALU.subtract), [wk(1), wk(5)], [wk(1)])
            P.op("dve", lambda e, a=a, zre=zre, wre=wre: e.tensor_tensor_scan(out=zre[:], data0=Rt[:, a, :], data1=wre[:], initial=zin[:, a, 0:1], op0=ALU.mult, op1=ALU.add),
                 [rk, wk(0), ("zin", a)], [wk(2)])
            P.op("dve", lambda e, a=a, zim=zim, wim=wim: e.tensor_tensor_scan(out=zim[:], data0=Rt[:, a, :], data1=wim[:], initial=zin[:, a, 1:2], op0=ALU.mult, op1=ALU.add),
                 [rk, wk(1), ("zin", a)], [wk(3)])
            if ti < 7:
                P.op("dve", lambda e, a=a, zim=zim: e.tensor_scalar(out=tA[:, 0:1], in0=zim[:, TT - 1:TT], scalar1=s5[:, a:a + 1], scalar2=None, op0=ALU.mult),
                     [wk(3), K_], ["cA"])
                P.op("dve", lambda e, a=a, zre=zre: e.tensor_scalar(out=tA[:, 1:2], in0=zre[:, TT - 1:TT], scalar1=s5[:, a:a + 1], scalar2=None, op0=ALU.mult),
                     [wk(2), K_], ["cB"])
                P.op("dve", lambda e, a=a, zre=zre: e.scalar_tensor_tensor(out=zin[:, a, 0:1], in0=zre[:, TT - 1:TT], scalar=c5[:, a:a + 1], in1=tA[:, 0:1], op0=ALU.mult, op1=ALU.subtract),
                     [wk(2), K_, "cA"], [("zin", a)])
                P.op("dve", lambda e, a=a, zim=zim: e.scalar_tensor_tensor(out=zin[:, a, 1:2], in0=zim[:, TT - 1:TT], scalar=c5[:, a:a + 1], in1=tA[:, 1:2], op0=ALU.mult, op1=ALU.add),
                     [wk(3), K_, "cB"], [("zin", a)])
            P.op("pool", lambda e, cosA=cosA, zre=zre, t1=t1: e.tensor_tensor(out=t1[:], in0=cosA, in1=zre[:], op=ALU.mult), [ek, wk(2)], [wk(4)])
            P.op("pool", lambda e, sinA=sinA, zim=zim, t2=t2: e.tensor_tensor(out=t2[:], in0=sinA, in1=zim[:], op=ALU.mult), [ek, wk(3)], [wk(5)])
            P.op("pool", lambda e, xb=xb, t1=t1, t2=t2: e.tensor_tensor(out=xb[0][:], in0=t1[:], in1=t2[:], op=ALU.subtract), [wk(4), wk(5)], [("Xb", wi % 2, 0)])
            P.op("pool", lambda e, cosA=cosA, zim=zim, t1=t1: e.tensor_tensor(out=t1[:], in0=cosA, in1=zim[:], op=ALU.mult), [ek, wk(3)], [wk(4)])
            P.op("pool", lambda e, sinA=sinA, zre=zre, t2=t2: e.tensor_tensor(out=t2[:], in0=sinA, in1=zre[:], op=ALU.mult), [ek, wk(2)], [wk(5)])
            P.op("pool", lambda e, xb=xb, t1=t1, t2=t2: e.tensor_tensor(out=xb[1][:], in0=t1[:], in1=t2[:], op=ALU.add), [wk(4), wk(5)], [("Xb", wi % 2, 1)])
            by = 4 + a % 4
            P.op("pe", lambda e, a=a, xb=xb, by=by: e.matmul(B.psb[by][0:32, :], lhsT=Cre[:, a, :], rhs=xb[0][:], start=True, stop=False), ["Cre", ("Xb", wi % 2, 0)], [("ps", by)])
            P.op("pe", lambda e, a=a, xb=xb, by=by: e.matmul(B.psb[by][0:32, :], lhsT=Cimn[:, a, :], rhs=xb[1][:], start=False, stop=True), ["Cimn", ("Xb", wi % 2, 1)], [("ps", by)])
            P.op("dve", lambda e, a=a, U=U, by=by, yfa=yfa: e.scalar_tensor_tensor(out=yfa[:], in0=U[:, a, :], scalar=d_sb[:, a:a + 1], in1=B.psb[by][0:32, :], op0=ALU.mult, op1=ALU.add),
                 [uk, "d", ("ps", by)], [("yf", wi % 2)])
            P.op("act", lambda e, a=a, Y=Y, yfa=yfa: e.activation(out=Y[:, a, :], in_=yfa[:], func=AF.Gelu), [("yf", wi % 2)], [yk])
        P.dma("sp", out[:, tok].rearrange("(a q) t -> q a t", q=32), Y[:], reads=[yk], key=yk)
    P.end()
    if own:
        P.finish()
    return P.nc


def colv(v, kc):
    return np.ascontiguousarray(v.reshape(kc, 128).T)
def mla_maps(projT, positions, prm, l):
    maps = []
    w_uq = prm["mla_w_uq"][l]; w_ukv = prm["mla_w_ukv"][l]
    inv = (10000.0 ** (-np.arange(16, dtype=np.float32) / 16)).astype(np.float32)
    cst = np.stack([np.concatenate([inv, inv]) / np.float32(2 * np.pi), np.concatenate([-np.ones(16, np.float32), np.ones(16, np.float32)])], 1).astype(np.float32)
    tri = np.triu(np.ones((128, 128), np.float32))
    for c in range(8):
        b, hh = c // 2, c % 2
        pt = projT[b]
        heads = [hh * 4 + i for i in range(4)]
        wq = np.concatenate([w_uq[:, g * 96:(g + 1) * 96] for g in heads], 1)
        wqs = np.concatenate([np.concatenate([np.zeros((512, 64), np.float32), w_uq[:, g * 96 + 80:g * 96 + 96], w_uq[:, g * 96 + 64:g * 96 + 80]], 1) for g in heads], 1)
        wkn = np.concatenate([w_ukv[:, g * 128:g * 128 + 64] for g in heads], 1)
        wv = np.concatenate([w_ukv[:, g * 128 + 64:g * 128 + 128] for g in heads], 1)
        kr = pt[1280:1312]
        maps.append(dict(qlatT=np.ascontiguousarray(pt[512:1024]), kvlatT=np.ascontiguousarray(pt[1024:1280]),
                         krT=np.ascontiguousarray(kr), krsT=np.ascontiguousarray(np.concatenate([kr[16:32], kr[0:16]], 0)),
                         pos=np.ascontiguousarray(np.broadcast_to(positions[b].astype(np.float32)[None, :], (32, 4096))),
                         cst=cst, tri=tri, gq=colv(prm["mla_q_norm"][l], 4), gkv=colv(prm["mla_kv_norm"][l], 2),
                         wq=np.ascontiguousarray(wq), wqs=np.ascontiguousarray(wqs), wkn=np.ascontiguousarray(wkn), wv=np.ascontiguousarray(wv)))
    return maps
def hg_maps(projT, prm, l):
    maps = []
    cm = np.tile(np.triu(np.ones((64, 64), np.float32)), (1, 8))
    idn = np.eye(128, dtype=np.float32)
    for c in range(8):
        b, hh = c // 2, c % 2
        pt = projT[b]
        r = slice(hh * 256, (hh + 1) * 256)
        lbp = np.ascontiguousarray(prm["hg_lb"][:, r].reshape(2, 2, 128).transpose(2, 0, 1))
        maps.append(dict(qT=np.ascontiguousarray(pt[1312:1824][r]), fT=np.ascontiguousarray(pt[1824:2336][r]),
                         itok=np.ascontiguousarray(pt[2336:2848][r].T), gT=np.ascontiguousarray(pt[2848:3360][r]),
                         lbp=lbp, gn=np.ascontiguousarray(prm["hg_g_norm"][l].reshape(128, 1)), cm=cm, idn=idn))
    return maps
def s5_maps(projT, prm, l):
    maps = []
    jrow = np.ascontiguousarray(np.broadcast_to(np.arange(512, dtype=np.float32)[None, :], (128, 512)))
    idn = np.eye(128, dtype=np.float32)
    for c in range(8):
        b, hh = c // 2, c % 2
        pt = projT[b]
        gs = np.arange(hh * 16, hh * 16 + 16).reshape(8, 2)
        def sp(v):
            return np.ascontiguousarray(v[gs].transpose(1, 2, 0).reshape(128, 8))
        lamre = sp(prm["ssm_lam_re"][l]); lamim = sp(prm["ssm_lam_im"][l])
        lstep = sp(np.broadcast_to(prm["ssm_log_step"][l][:, None], (32, 64)))
        def padb(v):
            o = np.zeros((2, 64, 8, 2, 16), np.float32)
            for gl in range(2):
                o[gl, :, :, gl, :] = v[gs[:, gl]].transpose(1, 0, 2)
            return o.reshape(128, 8, 32)
        def padc(v):
            o = np.zeros((2, 64, 8, 2, 16), np.float32)
            for gl in range(2):
                o[gl, :, :, gl, :] = v[gs[:, gl]].transpose(2, 0, 1)
            return o.reshape(128, 8, 32)
        dsk = np.ascontiguousarray(prm["ssm_d"][l][gs].transpose(1, 2, 0).reshape(32, 8))
        maps.append(dict(uT=np.ascontiguousarray(pt[0:512][hh * 256:(hh + 1) * 256]), lamre=lamre, lamim=lamim, lstep=lstep,
                         bre=padb(prm["ssm_b_re"][l]), bim=padb(prm["ssm_b_im"][l]), cre=padc(prm["ssm_c_re"][l]), cim=padc(prm["ssm_c_im"][l]),
                         dsk=dsk, jrow=jrow, idn=idn))
    return maps


def build_fused(L=2):
    P = Prog(bass.Bass("TRN2", target_bir_lowering=False))
    x_ext = P.dram("xT", [D, S], F32, "ExternalInput")
    P.ext_in.append(("", "xT"))
    out_ext = P.dram("xoT", [D, S], F32, "ExternalOutput")
    x_cur = x_ext
    for l in range(L):
        proj = P.dram(f"L{l}_proj", [NA, S], F32, "Internal")
        itok = P.dram(f"L{l}_itok", [S, 512], F32, "Internal")
        krs = P.dram(f"L{l}_krs", [32, S], F32, "Internal")
        ys = P.dram(f"L{l}_ys", [512, S], BF16, "Internal")
        om = P.dram(f"L{l}_om", [512, S], BF16, "Internal")
        oh = P.dram(f"L{l}_oh", [512, S], BF16, "Internal")
        build_A(P=P, io=IO(P, f"L{l}_A_", dict(xT=x_cur, projT=proj, itok=itok, krs=krs)), NS=4)
        for hh in range(2):
            r = slice(hh * 256, (hh + 1) * 256)
            build_MLA(P=P, io=IO(P, f"L{l}_M{hh}_", dict(qlatT=proj[512:1024], kvlatT=proj[1024:1280], krT=proj[1280:1312], krsT=krs, omT=om[r])))
            build_HG(l, P=P, io=IO(P, f"L{l}_H{hh}_", dict(qT=proj[1312:1824][r], fT=proj[1824:2336][r], gT=proj[2848:3360][r],
                                                           itok=itok[:, r], ohT=oh[r])))
            build_S5(P=P, io=IO(P, f"L{l}_S{hh}_", dict(uT=proj[0:512][r], ysT=ys[r])))
        last = (l == L - 1)
        x_next = out_ext if last else P.dram(f"L{l}_xo", [D, S], F32, "Internal")
        build_C(last, P=P, io=IO(P, f"L{l}_C_", dict(xT=x_cur, ysT=ys, omT=om, ohT=oh, xoT=x_next)), NS=4)
        x_cur = x_next
    P.finish()
    return P.nc, P.ext_in


def _forward_fused(inp):
    prm = {k: np.asarray(v) for k, v in inp.items()}
    x = prm["x"]
    mem = prm["mem"]
    L = prm["w_in"].shape[0]
    nc, ext_in = build_fused(L)
    maps = [dict(xT=np.ascontiguousarray(x[c // 2].T)) for c in range(8)]
    dummy = [np.zeros((NA, 8), np.float32) for _ in range(4)]
    for l in range(L):
        w_in = prm["w_in"][l]
        mm, hm, sm = mla_maps(dummy, prm["positions"], prm, l), hg_maps(dummy, prm, l), s5_maps(dummy, prm, l)
        gv = np.ascontiguousarray(np.stack([colv(prm["norm_mix"][l], 8), colv(prm["norm_cross"][l], 8), colv(prm["norm_mem"][l], 8),
                                            colv(prm["norm_ffn"][l], 8), colv(prm["norm_final"], 8)], axis=1))
        cw = dict(gv=gv, w_gate=w_in[:, NA:], w_glu=prm["ssm_w_glu"][l], w_mo=prm["mla_w_o"][l], w_ho=prm["hg_w_o"][l],
                  w_out=prm["w_out"][l], w_xq=prm["x_w_q"][l], w_xkv=prm["x_w_kv"][l], w_xo=prm["x_w_o"][l],
                  w_gu=prm["ffn_w_gate_up"][l], w_dn=prm["ffn_w_down"][l])
        cw = {k: np.ascontiguousarray(v) for k, v in cw.items()}
        aw = dict(wa=np.ascontiguousarray(w_in[:, :NA]), gm=colv(prm["norm_mix"][l], 8))
        for c in range(8):
            b = c // 2
            src = {f"L{l}_A_": aw, f"L{l}_C_": dict(memT=np.ascontiguousarray(mem[b].T), **cw)}
            for hh in range(2):
                src[f"L{l}_M{hh}_"] = mm[2 * b + hh]
                src[f"L{l}_H{hh}_"] = hm[2 * b + hh]
                src[f"L{l}_S{hh}_"] = sm[2 * b + hh]
            for pre, name in ext_in:
                if pre in src:
                    maps[c][pre + name] = src[pre][name]
    r = _run(nc, maps)
    out = np.empty(x.shape, np.float32)
    for b in range(4):
        out[b] = r[2 * b]["xoT"].T
    return out


def _run(nc, maps):
    res = run_bass_kernel_spmd(nc, maps, core_ids=list(range(8)))
    return res.results


def _forward(inp, dbg=None):
    prm = {k: np.asarray(v) for k, v in inp.items()}
    x = prm["x"]
    mem = prm["mem"]
    L = prm["w_in"].shape[0]
    xT = [np.ascontiguousarray(x[c // 2, (c % 2) * 2048:(c % 2 + 1) * 2048].T) for c in range(8)]
    memT = [np.ascontiguousarray(mem[b].T) for b in range(4)]
    for l in range(L):
        w_in = prm["w_in"][l]
        wa = np.ascontiguousarray(w_in[:, :NA])
        gm = colv(prm["norm_mix"][l], 8)
        r = _run(build_A(), [dict(xT=xT[c], wa=wa, gm=gm) for c in range(8)])
        projT = [np.concatenate([r[2 * b]["projT"], r[2 * b + 1]["projT"]], axis=1) for b in range(4)]
        if dbg:
            dbg(f"L{l}_proj", projT)
        r_m = _run(build_MLA(), mla_maps(projT, prm["positions"], prm, l))
        r_h = _run(build_HG(l), hg_maps(projT, prm, l))
        r_s = _run(build_S5(), s5_maps(projT, prm, l))
        del projT

        def halves(rr, name):
            full = [np.concatenate([rr[2 * b][name], rr[2 * b + 1][name]], axis=0) for b in range(4)]
            return [np.ascontiguousarray(full[c // 2][:, (c % 2) * 2048:(c % 2 + 1) * 2048]) for c in range(8)]
        ys, om, oh = halves(r_s, "ysT"), halves(r_m, "omT"), halves(r_h, "ohT")
        gv = np.ascontiguousarray(np.stack([colv(prm["norm_mix"][l], 8), colv(prm["norm_cross"][l], 8), colv(prm["norm_mem"][l], 8),
                                            colv(prm["norm_ffn"][l], 8), colv(prm["norm_final"], 8)], axis=1))
        wts = dict(gv=gv, w_gate=np.ascontiguousarray(w_in[:, NA:]), w_glu=prm["ssm_w_glu"][l], w_mo=prm["mla_w_o"][l], w_ho=prm["hg_w_o"][l],
                   w_out=prm["w_out"][l], w_xq=prm["x_w_q"][l], w_xkv=prm["x_w_kv"][l], w_xo=prm["x_w_o"][l],
                   w_gu=prm["ffn_w_gate_up"][l], w_dn=prm["ffn_w_down"][l])
        wts = {k: np.ascontiguousarray(v) for k, v in wts.items()}
        maps = [dict(xT=xT[c], ysT=ys[c], omT=om[c], ohT=oh[c], memT=memT[c // 2], **wts) for c in range(8)]
        r = _run(build_C(l == L - 1), maps)
        xT = [r[c]["xoT"] for c in range(8)]
        if dbg:
            dbg(f"L{l}_xout", xT)
    out = np.empty(x.shape, np.float32)
    for c in range(8):
        out[c // 2, (c % 2) * 2048:(c % 2 + 1) * 2048] = xT[c].T
    return out


def kernel(**inputs):
    return _forward_fused(inputs)
```
